# Optimizing a Trainium2 kernel written in Bass

```python
import jax, jax.numpy as jnp
from jax import lax
import numpy as np

D_MODEL = 1024
BATCH = 4
SEQ = 4096
DEPTH = 1

POOL_WINDOWS = (2, 4, 8, 16)
N_POOL_GROUPS = len(POOL_WINDOWS)
D_POOL = D_MODEL // 2
POOL_GROUP = D_POOL // N_POOL_GROUPS
D_RNN = D_MODEL
N_RNN_HEADS = 8
RNN_HEAD = D_RNN // N_RNN_HEADS
CONV_WIDTH = 4
LRU_C = 8.0
N_BRANCHES = 2
D_IN = D_POOL + 2 * D_RNN + N_BRANCHES * D_MODEL
D_FF = -(-8 * D_MODEL // (3 * 256)) * 256
NORM_EPS = 1e-6

kernel_name = "hybrid_pool_rglru_gated_block"


def rmsnorm(x, g):
    xf = x.astype(jnp.float32)
    y = xf * lax.rsqrt(jnp.mean(xf * xf, axis=-1, keepdims=True) + NORM_EPS)
    return (y * g.astype(jnp.float32)).astype(x.dtype)


def pool_mixer(u, w_grp, scale):
    B, S, _ = u.shape
    uf = u.astype(jnp.float32).reshape(B, S, N_POOL_GROUPS, POOL_GROUP)
    c = jnp.cumsum(uf, axis=1)
    pos = jnp.arange(S)
    outs = []
    for g, w in enumerate(POOL_WINDOWS):
        cg = c[:, :, g]
        c_lo = jnp.pad(cg[:, : S - w], ((0, 0), (w, 0), (0, 0)))
        count = jnp.minimum(pos + 1, w).astype(jnp.float32)[None, :, None]
        outs.append((cg - c_lo) / count - uf[:, :, g])
    pooled = jnp.stack(outs, axis=2).astype(u.dtype)
    mixed = jnp.einsum("bsgc,gcd->bsgd", pooled, w_grp)
    return mixed.reshape(B, S, D_POOL) * scale


def causal_depthwise_conv(u, w, b):
    S = u.shape[1]
    up = jnp.pad(u, ((0, 0), (CONV_WIDTH - 1, 0), (0, 0)))
    y = b
    for k in range(CONV_WIDTH):
        y = y + up[:, k : k + S] * w[k]
    return y


def rg_lru(v, w_a, b_a, w_x, b_x, lam):
    B, S, _ = v.shape
    vh = v.reshape(B, S, N_RNN_HEADS, RNN_HEAD)
    r = jax.nn.sigmoid((jnp.einsum("bshi,hij->bshj", vh, w_a) + b_a).astype(jnp.float32)).reshape(B, S, D_RNN)
    i = jax.nn.sigmoid((jnp.einsum("bshi,hij->bshj", vh, w_x) + b_x).astype(jnp.float32)).reshape(B, S, D_RNN)
    log_a = -LRU_C * r * jax.nn.softplus(-lam.astype(jnp.float32))
    a = jnp.exp(log_a)
    b = jnp.sqrt(-jnp.expm1(2.0 * log_a)) * i * v.astype(jnp.float32)

    def combine(left, right):
        a1, b1 = left
        a2, b2 = right
        return a1 * a2, a2 * b1 + b2

    _, h = lax.associative_scan(combine, (a, b), axis=1)
    return h.astype(v.dtype)


def setup_inputs(seed: int = 0) -> dict:
    key = jax.random.key(seed)
    ks = jax.random.split(key, 22)
    f32 = jnp.float32
    L = DEPTH

    def nrm(k, shape, fan_in):
        return jax.random.normal(k, shape, f32) * fan_in ** -0.5

    def small(k, shape, s=0.02):
        return jax.random.normal(k, shape, f32) * s

    x = jax.random.normal(ks[0], (BATCH, SEQ, D_MODEL), f32)
    norm_mix = 1.0 + small(ks[1], (L, D_MODEL))
    w_in = nrm(ks[2], (L, D_MODEL, D_IN), D_MODEL)
    w_pool_grp = nrm(ks[3], (L, N_POOL_GROUPS, POOL_GROUP, POOL_GROUP), POOL_GROUP)
    pool_scale = 1.0 + small(ks[4], (L, D_POOL))
    w_pool_out = nrm(ks[5], (L, D_POOL, D_MODEL), D_POOL)
    conv_w = nrm(ks[6], (L, CONV_WIDTH, D_RNN), CONV_WIDTH)
    conv_b = small(ks[7], (L, D_RNN))
    w_rg_a = nrm(ks[8], (L, N_RNN_HEADS, RNN_HEAD, RNN_HEAD), RNN_HEAD)
    b_rg_a = small(ks[9], (L, N_RNN_HEADS, RNN_HEAD))
    w_rg_x = nrm(ks[10], (L, N_RNN_HEADS, RNN_HEAD, RNN_HEAD), RNN_HEAD)
    b_rg_x = small(ks[11], (L, N_RNN_HEADS, RNN_HEAD))
    a_c = jax.random.uniform(ks[12], (L, D_RNN), f32, minval=0.9, maxval=0.999)
    a0 = a_c ** (1.0 / LRU_C)
    lru_lambda = jnp.log(a0) - jnp.log1p(-a0)
    w_rnn_out = nrm(ks[13], (L, D_RNN, D_MODEL), D_RNN)
    w_o = nrm(ks[14], (L, D_MODEL, D_MODEL), D_MODEL)
    norm_ffn = 1.0 + small(ks[15], (L, D_MODEL))
    w_ffn_in = nrm(ks[16], (L, D_MODEL, 2 * D_FF), D_MODEL)
    w_ffn_out = nrm(ks[17], (L, D_FF, D_MODEL), D_FF)
    norm_final = 1.0 + small(ks[18], (D_MODEL,))
    return {"x": x, "norm_mix": norm_mix, "w_in": w_in, "w_pool_grp": w_pool_grp,
            "pool_scale": pool_scale, "w_pool_out": w_pool_out, "conv_w": conv_w, "conv_b": conv_b,
            "w_rg_a": w_rg_a, "b_rg_a": b_rg_a, "w_rg_x": w_rg_x, "b_rg_x": b_rg_x,
            "lru_lambda": lru_lambda, "w_rnn_out": w_rnn_out, "w_o": w_o, "norm_ffn": norm_ffn,
            "w_ffn_in": w_ffn_in, "w_ffn_out": w_ffn_out, "norm_final": norm_final}


def reference(x, norm_mix, w_in, w_pool_grp, pool_scale, w_pool_out, conv_w, conv_b,
              w_rg_a, b_rg_a, w_rg_x, b_rg_x, lru_lambda, w_rnn_out, w_o, norm_ffn,
              w_ffn_in, w_ffn_out, norm_final):
    B, S, _ = x.shape
    for l in range(DEPTH):
        h = rmsnorm(x, norm_mix[l])
        proj = h @ w_in[l]
        o1 = D_POOL
        o2 = o1 + D_RNN
        o3 = o2 + D_RNN
        u_pool = proj[..., :o1]
        u_rnn = proj[..., o1:o2]
        u_gate = proj[..., o2:o3]
        g_merge = jax.nn.sigmoid(proj[..., o3:].reshape(B, S, N_BRANCHES, D_MODEL))

        y_pool = pool_mixer(u_pool, w_pool_grp[l], pool_scale[l]) @ w_pool_out[l]

        v = causal_depthwise_conv(u_rnn, conv_w[l], conv_b[l])
        hr = rg_lru(v, w_rg_a[l], b_rg_a[l], w_rg_x[l], b_rg_x[l], lru_lambda[l])
        y_rnn = (hr * jax.nn.gelu(u_gate)) @ w_rnn_out[l]

        mix = g_merge[:, :, 0] * y_pool + g_merge[:, :, 1] * y_rnn
        x = x + mix @ w_o[l]

        h = rmsnorm(x, norm_ffn[l])
        gu = h @ w_ffn_in[l]
        gate, up = gu[..., :D_FF], gu[..., D_FF:]
        x = x + (jax.nn.silu(gate) * up) @ w_ffn_out[l]
    return rmsnorm(x, norm_final)
```

```python
from contextlib import ExitStack

import numpy as np
import concourse.bass as bass
import concourse.mybir as mybir
from concourse.bass_utils import run_bass_kernel_spmd

F32 = mybir.dt.float32
BF16 = mybir.dt.bfloat16
AF = mybir.ActivationFunctionType
ALU = mybir.AluOpType

D = 1024
TOK = 2048
NT = 1024
D_IN = 4608
D_FF = 2816
NF = 22
EPS = 1e-6


class _Op:
    __slots__ = ("eng", "fn", "deps", "sig", "sem_key", "inc", "value")

    def __init__(self, eng, fn):
        self.eng = eng
        self.fn = fn
        self.deps = []
        self.sig = False
        self.sem_key = None
        self.inc = 1
        self.value = None


class Sched:
    def __init__(self, same_engine_sync=True):
        self.ops = []
        self.last_w = {}
        self.readers = {}
        self.same_engine_sync = same_engine_sync
        self.dma_keys = []
        self.final_counters = {}

    def op(self, eng, fn, reads=(), writes=(), dma_sem=None):
        o = _Op(eng, fn)
        if dma_sem is not None:
            o.sem_key = dma_sem
            o.inc = 16
            o.sig = True
            if dma_sem not in self.dma_keys:
                self.dma_keys.append(dma_sem)
        deps = []
        for k in reads:
            w = self.last_w.get(k)
            if w is not None:
                deps.append(w)
        for k in writes:
            w = self.last_w.get(k)
            if w is not None:
                deps.append(w)
            deps.extend(self.readers.get(k, ()))
        seen = set()
        for d in deps:
            if d is o or id(d) in seen:
                continue
            seen.add(id(d))
            if d.sem_key is None and d.eng == eng:
                if eng == "pe" or not self.same_engine_sync:
                    continue
            o.deps.append(d)
            d.sig = True
        for k in reads:
            self.readers.setdefault(k, []).append(o)
        for k in writes:
            self.last_w[k] = o
            self.readers[k] = []
        self.ops.append(o)
        return o

    def finalize(self):
        counters = {}
        for o in self.ops:
            if o.sig:
                key = ("E", o.eng) if o.sem_key is None else ("D", o.sem_key)
                counters[key] = counters.get(key, 0) + o.inc
                o.value = counters[key]
        self.final_counters = counters

    def emit_engine(self, eng_name, eng, sems, dma_sems):
        waited = {}
        for o in self.ops:
            if o.eng != eng_name:
                continue
            for d in o.deps:
                key = ("E", d.eng) if d.sem_key is None else ("D", d.sem_key)
                if waited.get(key, 0) >= d.value:
                    continue
                sem = sems[d.eng] if d.sem_key is None else dma_sems[d.sem_key]
                eng.wait_ge(sem, d.value)
                waited[key] = d.value
            ins = o.fn(eng)
            if o.sig:
                sem = sems[o.eng] if o.sem_key is None else dma_sems[o.sem_key]
                ins.then_inc(sem, o.inc)


PF_D = 3


def build_program(prefetch=None):
    load_log = []
    nc = bass.Bass("TRN2", target_bir_lowering=False)

    def din(name, shape):
        return nc.dram_tensor(name, list(shape), F32, kind="ExternalInput").ap()

    xc = din("xc", [TOK, D])
    xp = din("xp", [TOK, D])
    flag_d = din("flag", [128, 1])
    invc_d = din("invc", [128, 64])
    idn_d = din("idn", [128, 128])
    norm_mix = din("norm_mix", [D])
    w_in = din("w_in", [D, D_IN])
    w_pool_grp = din("w_pool_grp", [4, 128, 128])
    pool_scale = din("pool_scale", [512])
    w_pool_out = din("w_pool_out", [512, D])
    conv_w = din("conv_w", [4, D])
    conv_b = din("conv_b", [D])
    w_rg_a = din("w_rg_a", [8, 128, 128])
    b_rg_a = din("b_rg_a", [8, 128])
    w_rg_x = din("w_rg_x", [8, 128, 128])
    b_rg_x = din("b_rg_x", [8, 128])
    lru_lambda = din("lru_lambda", [D])
    w_rnn_out = din("w_rnn_out", [D, D])
    w_o = din("w_o", [D, D])
    norm_ffn = din("norm_ffn", [D])
    w_ffn_in = din("w_ffn_in", [D, 2 * D_FF])
    w_ffn_out = din("w_ffn_out", [D_FF, D])
    norm_final = din("norm_final", [D])
    out_d = nc.dram_tensor("out", [TOK, D], F32, kind="ExternalOutput").ap()

    w_in_v = w_in.rearrange("(kc p) n -> p kc n", p=128)
    w_ffn_in_v = w_ffn_in.rearrange("(kc p) n -> p kc n", p=128)
    w_ffn_out_v = w_ffn_out.rearrange("(f p) n -> p f n", p=128)
    w_o_v = w_o.rearrange("(kc p) n -> p kc n", p=128)
    w_rnn_out_v = w_rnn_out.rearrange("(kc p) n -> p kc n", p=128)
    w_pool_out_v = w_pool_out.rearrange("(kc p) n -> p kc n", p=128)

    S = Sched(same_engine_sync=True)

    with ExitStack() as es:
        def sb(name, shape, dt=F32):
            return es.enter_context(nc.sbuf_tensor(name, list(shape), dt))

        xs0 = sb("xs0", [128, 2, D])
        hbf = sb("hbf", [128, 2, D], BF16)
        X = sb("X", [128, 8, D])
        h1T = sb("h1T", [128, 8, NT], BF16)
        h2T = sb("h2T", [128, 8, NT], BF16)
        R = sb("R", [128, 20, NT], BF16)
        WA = sb("WA", [128, 8, D], BF16)
        rga = sb("rga", [128, 8, 128], BF16)
        rgx = sb("rgx", [128, 8, 128], BF16)
        grp = sb("grp", [128, 4, 128], BF16)
        ring = sb("ring", [128, 8, 8, 128], BF16)
        NSCR = 17
        scr = sb("scr", [128, NSCR, 528])
        vb = sb("vb", [128, 2, 512], BF16)
        pl = sb("pl", [128, 2, 512], BF16)
        g1s = sb("g1s", [128, D])
        g2s = sb("g2s", [128, D])
        gFs = sb("gFs", [128, D])
        identf = sb("identf", [128, 128])
        ident = sb("ident", [128, 128], BF16)
        cw = sb("cw", [128, 4, 8])
        cbt = sb("cbt", [128, 8])
        hba = sb("hba", [128, 8])
        hbx = sb("hbx", [128, 8])
        lam = sb("lam", [128, 8])
        s_t = sb("s_t", [128, 8])
        hs_t = sb("hs_t", [128, 8])
        psc = sb("psc", [128, 4])
        invct = sb("invct", [128, 4, 16])
        flagt = sb("flagt", [128, 1])
        eps_t = sb("eps_t", [128, 1])
        q25 = sb("q25", [128, 1])
        one_t = sb("one_t", [128, 1])
        uh = sb("uh", [128, 8, 3])
        ph = sb("ph", [128, 4, 15])
        hst = sb("hst", [128, 8])
        NSS = 16
        ssq = sb("ssq", [128, NSS])
        sd = sb("sd", [128, NSS])
        rs = sb("rs", [128, NSS])
        umb = sb("umb", [128, 1])

        ps = [es.enter_context(nc.psum_tensor("ps%d" % i, [128, 512], F32)) for i in range(7)]
        pst = es.enter_context(nc.psum_tensor("pst", [128, 8, 128], BF16))

        cnt = {"ring": 0, "bank": 0, "ss": 0, "xs0": 0, "hbf": 0, "unit": 0, "plu": 0}

        def nbank():
            b = cnt["bank"] % 7
            cnt["bank"] += 1
            return b

        def nss():
            n = cnt["ss"] % NSS
            cnt["ss"] += 1
            return n

        def nunit():
            p = cnt["unit"] % 2
            cnt["unit"] += 1
            return p

        views = {"in": w_in_v, "po": w_pool_out_v, "ro": w_rnn_out_v, "fi": w_ffn_in_v}
        NRING = 8
        lstate = {"emitted": 0, "use": 0}
        open_slots = set()

        def emit_load(desc, slot):
            v, c0, K = desc
            src = views[v][:, :, c0:c0 + 128]
            S.op("pool", lambda e: e.dma_start(out=ring[:, slot, 0:K, :], in_=src),
                 writes=[("ring", slot)], dma_sem=("ring", slot))
            open_slots.add(slot)

        def pump_loads(upto):
            while lstate["emitted"] <= min(upto, len(prefetch) - 1):
                k = lstate["emitted"]
                if (k % NRING) in open_slots:
                    break
                emit_load(prefetch[k], k % NRING)
                lstate["emitted"] += 1

        def load_block(v, c0, K):
            desc = (v, c0, K)
            i = lstate["use"]
            lstate["use"] += 1
            load_log.append(desc)
            if prefetch is None:
                assert (i % NRING) not in open_slots, "ring slot still in use"
                emit_load(desc, i % NRING)
            else:
                assert prefetch[i] == desc
                pump_loads(i + PF_D)
                assert lstate["emitted"] > i, "load %d could not be issued: ring slot busy" % i
            return i % NRING

        def close_block(slot):
            open_slots.discard(slot)
            if prefetch is not None:
                pump_loads(lstate["use"] - 1 + PF_D)

        def proj(bank, slot, hT, hkey, half, K=8):
            rk = [(hkey, 4 * half + i) for i in range(4)]
            for kc in range(K):
                S.op("pe", lambda e, bank=bank, slot=slot, kc=kc, half=half, hT=hT, K=K: e.matmul(
                    ps[bank][:, :], lhsT=ring[:, slot, kc, :], rhs=hT[:, kc, half * 512:(half + 1) * 512],
                    start=(kc == 0), stop=(kc == K - 1)),
                    reads=[("ring", slot)] + rk, writes=[("ps", bank)])

        def norm_core(src_ap, src_key, gtile, gkey, out_ap, out_key, jhb, extra_reads=()):
            n = nss()
            S.op("act", lambda e, n=n: e.activation(out=hbf[:, jhb, :], in_=src_ap, func=AF.Square, accum_out=ssq[:, n:n + 1]),
                 reads=[src_key], writes=[("ssq", n), ("hbf", jhb)])
            S.op("act", lambda e, n=n: e.activation(out=sd[:, n:n + 1], in_=ssq[:, n:n + 1], func=AF.Sqrt,
                                                    scale=1.0 / D, bias=eps_t[:]),
                 reads=[("ssq", n), "m_eps"], writes=[("sd", n)])
            def fin():
                S.op("dve", lambda e, n=n: e.reciprocal(out=rs[:, n:n + 1], in_=sd[:, n:n + 1]),
                     reads=[("sd", n)], writes=[("rs", n)])
                S.op("dve", lambda e, n=n: e.scalar_tensor_tensor(out=out_ap, in0=src_ap, scalar=rs[:, n:n + 1], in1=gtile[:],
                                                                  op0=ALU.mult, op1=ALU.mult),
                     reads=[src_key, ("rs", n), gkey] + list(extra_reads), writes=[out_key])
            return fin

        def transposes(hb, hT, hkey, t):
            for k in range(8):
                S.op("pe", lambda e, k=k, hb=hb: e.transpose(out=pst[:, k, :], in_=hbf[:, hb, k * 128:(k + 1) * 128],
                                                             identity=ident[:]),
                     reads=[("hbf", hb), "ident"], writes=[("pst",)])
            S.op("act", lambda e, t=t, hT=hT: e.activation(out=hT[:, :, t * 128:(t + 1) * 128], in_=pst[:], func=AF.Copy),
                 reads=[("pst",)], writes=[(hkey, t)])

        def small_load(dst, src, key, slow=False):
            kw = {"allow_slow_non_contiguous": True} if slow else {}
            S.op("sp", lambda e: e.dma_start(out=dst, in_=src, **kw), writes=[key], dma_sem=("c", key))

        small_load(identf[:], idn_d, "identf")
        small_load(g1s[:], norm_mix.partition_broadcast(128), "g1s")

        def late_loads():
            small_load(flagt[:], flag_d, "flagt")
            small_load(invct[:], invc_d.rearrange("p (g t) -> p g t", g=4), "invct")
            for k in range(4):
                small_load(cw[:, k, :], conv_w[k, :].rearrange("(j p) -> p j", p=128), "cw%d" % k, slow=True)
            small_load(cbt[:], conv_b.rearrange("(j p) -> p j", p=128), "cbt", slow=True)
            small_load(hba[:], b_rg_a.rearrange("j p -> p j"), "hba", slow=True)
            small_load(hbx[:], b_rg_x.rearrange("j p -> p j"), "hbx", slow=True)
            small_load(lam[:], lru_lambda.rearrange("(j p) -> p j", p=128), "lam", slow=True)
            small_load(psc[:], pool_scale.rearrange("(g p) -> p g", p=128), "psc", slow=True)
            small_load(g2s[:], norm_ffn.partition_broadcast(128), "g2s")
            small_load(gFs[:], norm_final.partition_broadcast(128), "gFs")
        S.op("pool", lambda e: e.dma_start(out=rga[:], in_=w_rg_a.rearrange("j i o -> i j o")), writes=["rga"], dma_sem=("c", "rga"))
        S.op("pool", lambda e: e.dma_start(out=rgx[:], in_=w_rg_x.rearrange("j i o -> i j o")), writes=["rgx"], dma_sem=("c", "rgx"))
        S.op("pool", lambda e: e.dma_start(out=grp[:], in_=w_pool_grp.rearrange("g c d -> c g d")), writes=["grp"], dma_sem=("c", "grp"))

        S.op("dve", lambda e: e.memset(eps_t[:], EPS), writes=["m_eps"])
        S.op("dve", lambda e: e.memset(q25[:], 0.25), writes=["m_q25"])
        S.op("dve", lambda e: e.memset(one_t[:], 1.0), writes=["m_one"])
        S.op("dve", lambda e: e.memset(uh[:], 0.0), writes=[("uh", j) for j in range(8)])
        S.op("dve", lambda e: e.memset(ph[:], 0.0), writes=[("ph", g) for g in range(4)])
        S.op("dve", lambda e: e.memset(hst[:], 0.0), writes=[("hst", j) for j in range(8)])
        S.op("dve", lambda e: e.tensor_copy(out=ident[:], in_=identf[:]), reads=["identf"], writes=["ident"])
        def late_consts():
            S.op("dve", lambda e: e.tensor_scalar(out=hba[:], in0=hba[:], scalar1=0.5, scalar2=None, op0=ALU.mult),
                 reads=["hba"], writes=["hba"])
            S.op("dve", lambda e: e.tensor_scalar(out=hbx[:], in0=hbx[:], scalar1=0.5, scalar2=None, op0=ALU.mult),
                 reads=["hbx"], writes=["hbx"])
            S.op("act", lambda e: e.activation(out=lam[:], in_=lam[:], func=AF.Exp, scale=-1.0), reads=["lam"], writes=["lam"])
            S.op("act", lambda e: e.activation(out=lam[:], in_=lam[:], func=AF.Ln, bias=one_t[:]), reads=["lam", "m_one"], writes=["lam"])
            S.op("dve", lambda e: e.tensor_scalar(out=s_t[:], in0=lam[:], scalar1=-8.0, scalar2=None, op0=ALU.mult),
                 reads=["lam"], writes=["s_t"])
            S.op("dve", lambda e: e.tensor_scalar(out=hs_t[:], in0=lam[:], scalar1=-4.0, scalar2=None, op0=ALU.mult),
                 reads=["lam"], writes=["hs_t"])
            S.op("dve", lambda e: e.memset(umb[:], 0.0),
                 reads=["identf", "flagt", "invct", "cw0", "cw1", "cw2", "cw3", "cbt", "hba", "hbx", "psc", "g1s", "g2s", "gFs", "rga", "rgx", "grp",
                        "m_eps", "m_q25", "m_one", "ident", "s_t", "hs_t"],
                 writes=["consts"])


        def stage0_items(src, r0, hT, hkey, use_x=False):
            items = []
            st = {"prev": None}

            def mk(t):
                def f():
                    hb = cnt["hbf"] % 2
                    cnt["hbf"] += 1
                    rows = src[r0 + t * 128:r0 + (t + 1) * 128, :]
                    if use_x:
                        land, lkey = X[:, t, :], ("X", t)
                    else:
                        sl = cnt["xs0"] % 2
                        cnt["xs0"] += 1
                        land, lkey = xs0[:, sl, :], ("xs0", sl)
                    if st["tq"]:
                        phb, pt = st["tq"].pop(0)
                        transposes(phb, hT, hkey, pt)
                    S.op("sp", lambda e: e.dma_start(out=land, in_=rows), writes=[lkey], dma_sem=lkey)
                    fin = norm_core(land, lkey, g1s, "g1s", hbf[:, hb, :], ("hbf", hb), hb)
                    if st["prev"] is not None:
                        pfin, phb2, pt2 = st["prev"]
                        pfin()
                        st["tq"].append((phb2, pt2))
                    st["prev"] = (fin, hb, t)
                return f

            st["tq"] = []
            for t in range(8):
                items.append(mk(t))

            def last1():
                if st["tq"]:
                    phb, pt = st["tq"].pop(0)
                    transposes(phb, hT, hkey, pt)
                pfin, phb2, pt2 = st["prev"]
                pfin()
                st["tq"].append((phb2, pt2))

            def last2():
                phb, pt = st["tq"].pop(0)
                transposes(phb, hT, hkey, pt)
            items.append(last1)
            items.append(last2)
            return items

        def rnn_rounds(segs):
            slots = {}

            def bufs(u):
                return dict(
                    ub=(scr[:, u % 2, :], ("scr", u % 2)),
                    v=(scr[:, 2 + u % 2, :], ("scr", 2 + u % 2)),
                    thr=(scr[:, 4, :], ("scr", 4)),
                    w=(scr[:, 5 + u % 3, :], ("scr", 5 + u % 3)),
                    a=(scr[:, 8 + u % 3, :], ("scr", 8 + u % 3)),
                    a2=(scr[:, 11 + u % 2, :], ("scr", 11 + u % 2)),
                    hs=(scr[:, 13, :], ("scr", 13)),
                )

            def phA(u):
                hT, hkey = segs[u // 16][0], segs[u // 16][1]
                j, half = (u % 16) // 2, u % 2
                if half == 0:
                    slots[j] = load_block("in", 512 + 128 * j, 8)
                ub, kub = bufs(u)["ub"]
                b0 = nbank()
                proj(b0, slots[j], hT, hkey, half)
                if half == 1:
                    close_block(slots[j])
                S.op("act", lambda e: e.activation(out=ub[:, 3:515], in_=ps[b0][:, :], func=AF.Copy),
                     reads=[("ps", b0)], writes=[kub])

            def phB(u):
                j, half = (u % 16) // 2, u % 2
                ub, kub = bufs(u)["ub"]
                v, kv = bufs(u)["v"]
                S.op("dve", lambda e: e.tensor_copy(out=ub[:, 0:3], in_=uh[:, j, :]), reads=[("uh", j)], writes=[kub])
                S.op("dve", lambda e: e.tensor_copy(out=uh[:, j, :], in_=ub[:, 512:515]), reads=[kub], writes=[("uh", j)])
                S.op("dve", lambda e: e.tensor_scalar(out=v[:, 0:512], in0=ub[:, 0:512], scalar1=cw[:, 0, j:j + 1],
                                                      scalar2=cbt[:, j:j + 1], op0=ALU.mult, op1=ALU.add),
                     reads=[kub, "consts"], writes=[kv])
                for k in range(1, 4):
                    S.op("dve", lambda e, k=k: e.scalar_tensor_tensor(
                        out=v[:, 0:512], in0=ub[:, k:k + 512], scalar=cw[:, k, j:j + 1], in1=v[:, 0:512],
                        op0=ALU.mult, op1=ALU.add), reads=[kub, kv], writes=[kv])

            def phVB(u):
                p = u % 2
                v, kv = bufs(u)["v"]
                S.op("act", lambda e: e.activation(out=vb[:, p, :], in_=v[:, 0:512], func=AF.Copy), reads=[kv], writes=[("vb", p)])

            def phC(u):
                j, half = (u % 16) // 2, u % 2
                p = u % 2
                B = bufs(u)
                v, kv = B["v"]
                thr, kthr = B["thr"]
                w, kw = B["w"]
                a, ka = B["a"]
                b1 = nbank()
                b2 = nbank()
                S.op("pe", lambda e: e.matmul(ps[b1][:, :], lhsT=rga[:, j, :], rhs=vb[:, p, :], start=True, stop=True),
                     reads=[("vb", p), "consts"], writes=[("ps", b1)])
                S.op("pe", lambda e: e.matmul(ps[b2][:, :], lhsT=rgx[:, j, :], rhs=vb[:, p, :], start=True, stop=True),
                     reads=[("vb", p), "consts"], writes=[("ps", b2)])
                S.op("act", lambda e: e.activation(out=thr[:, 0:512], in_=ps[b1][:, :], func=AF.Tanh,
                                                   scale=0.5, bias=hba[:, j:j + 1]),
                     reads=[("ps", b1)], writes=[kthr])
                S.op("act", lambda e: e.activation(out=w[:, 0:512], in_=ps[b2][:, :], func=AF.Tanh,
                                                   scale=0.5, bias=hbx[:, j:j + 1]),
                     reads=[("ps", b2)], writes=[kw])
                S.op("act", lambda e: e.activation(out=a[:, 0:512], in_=thr[:, 0:512], func=AF.Exp,
                                                   scale=hs_t[:, j:j + 1], bias=hs_t[:, j:j + 1]),
                     reads=[kthr], writes=[ka])
                S.op("dve", lambda e: e.scalar_tensor_tensor(out=w[:, 0:512], in0=w[:, 0:512], scalar=1.0,
                                                             in1=v[:, 0:512], op0=ALU.add, op1=ALU.mult),
                     reads=[kw, kv], writes=[kw])

            def phC2(u, part):
                B = bufs(u)
                a, ka = B["a"]
                a2, ka2 = B["a2"]
                if part == 0:
                    S.op("act", lambda e: e.activation(out=a2[:, 0:512], in_=a[:, 0:512], func=AF.Square),
                         reads=[ka], writes=[ka2])
                else:
                    S.op("act", lambda e: e.activation(out=a2[:, 0:512], in_=a2[:, 0:512], func=AF.Sqrt,
                                                       scale=-0.25, bias=q25[:]),
                         reads=[ka2], writes=[ka2])

            def phD(u):
                state_only, apply_flag = segs[u // 16][2], segs[u // 16][3]
                j, half = (u % 16) // 2, u % 2
                B = bufs(u)
                if apply_flag and half == 0:
                    S.op("dve", lambda e: e.tensor_scalar(out=hst[:, j:j + 1], in0=hst[:, j:j + 1], scalar1=flagt[:, 0:1],
                                                          scalar2=None, op0=ALU.mult),
                         reads=[("hst", j), "consts"], writes=[("hst", j)])
                w, kw = B["w"]
                a, ka = B["a"]
                a2, ka2 = B["a2"]
                hs, khs = B["hs"]
                S.op("dve", lambda e: e.tensor_tensor(out=a2[:, 0:512], in0=a2[:, 0:512], in1=w[:, 0:512], op=ALU.mult),
                     reads=[kw, ka2], writes=[ka2])
                S.op("dve", lambda e: e.tensor_tensor_scan(out=hs[:, 0:512], data0=a[:, 0:512], data1=a2[:, 0:512],
                                                           initial=hst[:, j:j + 1], op0=ALU.mult, op1=ALU.add),
                     reads=[ka, ka2, ("hst", j)], writes=[khs])
                S.op("dve", lambda e: e.tensor_copy(out=hst[:, j:j + 1], in_=hs[:, 511:512]), reads=[khs], writes=[("hst", j)])
                if not state_only:
                    S.op("dve", lambda e: e.tensor_tensor(
                        out=R[:, j, half * 512:(half + 1) * 512], in0=hs[:, 0:512], in1=R[:, j, half * 512:(half + 1) * 512],
                        op=ALU.mult), reads=[khs, ("R", j, half)], writes=[("R", j, half)])

            N = 16 * len(segs)

            def mk_round(r):
                def f(bg_exp=(), bg_sqrt=()):
                    if r < N:
                        phA(r)
                    if 0 <= r - 1 < N:
                        phB(r - 1)
                    if 0 <= r - 2 < N:
                        phC(r - 2)
                    for it in bg_exp:
                        it()
                    if 0 <= r - 2 < N and (r - 2) % 2 == 1:
                        phC2(r - 3, 0)
                        phC2(r - 3, 1)
                        phC2(r - 2, 0)
                        phC2(r - 2, 1)
                    if 0 <= r - 3 < N and (r - 3) % 2 == 1:
                        phD(r - 4)
                        phD(r - 3)
                    for it in bg_sqrt:
                        it()
                    if 0 <= r - 1 < N:
                        phVB(r - 1)
                return f

            return [mk_round(r) for r in range(N + 5)]

        def gelu_items(hT, hkey):
            items = []
            for j0 in range(0, 8, 2):
                def f(j0=j0):
                    for j in (j0, j0 + 1):
                        slot = load_block("in", 1536 + 128 * j, 8)
                        for half in range(2):
                            b0 = nbank()
                            proj(b0, slot, hT, hkey, half)
                            if half == 1:
                                close_block(slot)
                            S.op("act", lambda e, b0=b0, j=j, half=half: e.activation(
                                out=R[:, j, half * 512:(half + 1) * 512], in_=ps[b0][:, :], func=AF.Gelu_apprx_tanh),
                                reads=[("ps", b0)], writes=[("R", j, half)])
                items.append(f)
            return items

        def pool_items(hT, hkey, halves, halo_only, fixup, sb3):
            items = []
            slots = {}
            pend = []
            for g in range(4):
                for half in halves:
                    def f(g=g, half=half):
                        if g not in slots:
                            slots[g] = load_block("in", 128 * g, 8)
                        slot = slots[g]
                        w = 2 ** (g + 1)
                        p = cnt["plu"] % 2
                        cnt["plu"] += 1
                        pb, sA, sB = [scr[:, i, :] for i in sb3]
                        kpb, ksA, ksB = [("scr", i) for i in sb3]
                        b0 = nbank()
                        proj(b0, slot, hT, hkey, half)
                        if half == halves[-1]:
                            close_block(slot)
                        S.op("act", lambda e: e.activation(out=pb[:, 0:15], in_=ph[:, g, :], func=AF.Copy),
                             reads=[("ph", g)], writes=[kpb])
                        S.op("act", lambda e: e.activation(out=pb[:, 15:527], in_=ps[b0][:, :], func=AF.Copy),
                             reads=[("ps", b0)], writes=[kpb])
                        S.op("act", lambda e: e.activation(out=ph[:, g, :], in_=pb[:, 512:527], func=AF.Copy),
                             reads=[kpb], writes=[("ph", g)])
                        if halo_only:
                            return
                        S.op("dve", lambda e: e.tensor_tensor(out=sA[:, 1:527], in0=pb[:, 0:526], in1=pb[:, 1:527], op=ALU.add),
                             reads=[kpb], writes=[ksA])
                        sW, ksW = sA, ksA
                        if g >= 1:
                            S.op("dve", lambda e: e.tensor_tensor(out=sB[:, 3:527], in0=sA[:, 3:527], in1=sA[:, 1:525], op=ALU.add),
                                 reads=[ksA], writes=[ksB])
                            sW, ksW = sB, ksB
                        if g >= 2:
                            S.op("dve", lambda e: e.tensor_tensor(out=sA[:, 7:527], in0=sB[:, 7:527], in1=sB[:, 3:523], op=ALU.add),
                                 reads=[ksB], writes=[ksA])
                            sW, ksW = sA, ksA
                        if g >= 3:
                            S.op("dve", lambda e: e.tensor_tensor(out=sB[:, 15:527], in0=sA[:, 15:527], in1=sA[:, 7:519], op=ALU.add),
                                 reads=[ksA], writes=[ksB])
                            sW, ksW = sB, ksB
                        S.op("dve", lambda e: e.scalar_tensor_tensor(
                            out=pl[:, p, :], in0=sW[:, 15:527], scalar=1.0 / w, in1=pb[:, 15:527], op0=ALU.mult, op1=ALU.subtract),
                            reads=[ksW, kpb], writes=[("pl", p)])
                        if fixup and half == 0:
                            S.op("dve", lambda e: e.tensor_tensor(out=sW[:, 15:31], in0=sW[:, 15:31], in1=invct[:, g, :], op=ALU.mult),
                                 reads=[ksW, "consts"], writes=[ksW])
                            S.op("dve", lambda e: e.tensor_tensor(out=pl[:, p, 0:16], in0=sW[:, 15:31], in1=pb[:, 15:31],
                                                                  op=ALU.subtract),
                                 reads=[ksW, kpb], writes=[("pl", p)])
                        def part2():
                            b1 = nbank()
                            S.op("pe", lambda e: e.matmul(ps[b1][:, :], lhsT=grp[:, g, :], rhs=pl[:, p, :], start=True, stop=True),
                                 reads=[("pl", p), "consts"], writes=[("ps", b1)])
                            S.op("act", lambda e: e.activation(out=R[:, 8 + g, half * 512:(half + 1) * 512], in_=ps[b1][:, :],
                                                               func=AF.Identity, scale=psc[:, g:g + 1]),
                                 reads=[("ps", b1)], writes=[("R", 8 + g, half)])
                        pend.append(part2)

                    def wrapped(f=f):
                        prev = pend.pop(0) if pend else None
                        f()
                        if prev is not None:
                            prev()
                    items.append(wrapped)
            if not halo_only:
                def flush():
                    while pend:
                        pend.pop(0)()
                items.append(flush)
            return items

        def mix_block(m, hT, hkey):
            sGP = load_block("in", 2560 + 128 * m, 8)
            sGR = load_block("in", 3584 + 128 * m, 8)
            sPO = load_block("po", 128 * m, 4)
            sRO = load_block("ro", 128 * m, 8)
            for half in range(2):
                p = nunit()
                base = 7 * p
                thp, thr, t1, t2 = [scr[:, base + i, :] for i in range(4)]
                kthp, kthr, kt1, kt2 = [("scr", base + i) for i in range(4)]
                bGP, bGR, bYP, bYR = nbank(), nbank(), nbank(), nbank()
                proj(bGP, sGP, hT, hkey, half)
                proj(bGR, sGR, hT, hkey, half)
                for g in range(4):
                    S.op("pe", lambda e, g=g, half=half, bYP=bYP, sPO=sPO: e.matmul(
                        ps[bYP][:, :], lhsT=ring[:, sPO, g, :], rhs=R[:, 8 + g, half * 512:(half + 1) * 512],
                        start=(g == 0), stop=(g == 3)), reads=[("ring", sPO), ("R", 8 + g, half)], writes=[("ps", bYP)])
                for j in range(8):
                    S.op("pe", lambda e, j=j, half=half, bYR=bYR, sRO=sRO: e.matmul(
                        ps[bYR][:, :], lhsT=ring[:, sRO, j, :], rhs=R[:, j, half * 512:(half + 1) * 512],
                        start=(j == 0), stop=(j == 7)), reads=[("ring", sRO), ("R", j, half)], writes=[("ps", bYR)])
                if half == 1:
                    for sl_ in (sGP, sGR, sPO, sRO):
                        close_block(sl_)
                S.op("act", lambda e, thp=thp, bGP=bGP: e.activation(out=thp[:, 0:512], in_=ps[bGP][:, :], func=AF.Tanh, scale=0.5),
                     reads=[("ps", bGP)], writes=[kthp])
                S.op("act", lambda e, thr=thr, bGR=bGR: e.activation(out=thr[:, 0:512], in_=ps[bGR][:, :], func=AF.Tanh, scale=0.5),
                     reads=[("ps", bGR)], writes=[kthr])
                S.op("dve", lambda e, thp=thp, t1=t1, bYP=bYP: e.scalar_tensor_tensor(
                    out=t1[:, 0:512], in0=thp[:, 0:512], scalar=1.0, in1=ps[bYP][:, :], op0=ALU.add, op1=ALU.mult),
                    reads=[kthp, ("ps", bYP)], writes=[kt1])
                S.op("dve", lambda e, thr=thr, t2=t2, bYR=bYR: e.scalar_tensor_tensor(
                    out=t2[:, 0:512], in0=thr[:, 0:512], scalar=1.0, in1=ps[bYR][:, :], op0=ALU.add, op1=ALU.mult),
                    reads=[kthr, ("ps", bYR)], writes=[kt2])
                S.op("dve", lambda e, t1=t1, t2=t2, m=m, half=half: e.tensor_tensor(
                    out=R[:, 12 + m, half * 512:(half + 1) * 512], in0=t1[:, 0:512], in1=t2[:, 0:512], op=ALU.add),
                    reads=[kt1, kt2], writes=[("R", 12 + m, half)])

        WO_ORDER = [0, 1, 2, 3, 7, 4, 5, 6]

        def load_wo_part(m):
            kc = WO_ORDER[m]
            S.op("pool", lambda e: e.dma_start(out=WA[:, kc:kc + 1, :], in_=w_o_v[:, kc:kc + 1, :]),
                 writes=[("WA", kc)], dma_sem=("WA", kc))

        def wo_tile(src_rows, t, mid=None):
            S.op("sp", lambda e, t=t: e.dma_start(out=X[:, t, :], in_=src_rows), writes=[("X", t)], dma_sem=("X", t))
            half = t // 4
            for c in range(2):
                b = nbank()
                for m in range(8):
                    S.op("pe", lambda e, b=b, m=m, t=t, c=c: e.matmul(
                        ps[b][:, :], lhsT=R[:, 12 + m, t * 128:(t + 1) * 128], rhs=WA[:, m, c * 512:(c + 1) * 512],
                        start=(m == 0), stop=(m == 7)), reads=[("R", 12 + m, half), ("WA", m)], writes=[("ps", b)])
                S.op("dve", lambda e, b=b, t=t, c=c: e.scalar_tensor_tensor(
                    out=X[:, t, c * 512:(c + 1) * 512], in0=ps[b][:, :], scalar=0.5, in1=X[:, t, c * 512:(c + 1) * 512],
                    op0=ALU.mult, op1=ALU.add), reads=[("ps", b), ("X", t)], writes=[("X", t)])
            if mid is not None:
                mid()
            hb = cnt["hbf"] % 2
            cnt["hbf"] += 1
            fin = norm_core(X[:, t, :], ("X", t), g2s, "g2s", hbf[:, hb, :], ("hbf", hb), hb)
            return hb, fin

        GROUPS = [(0, 4), (4, 4), (8, 4), (12, 4), (16, 3), (19, 3)]

        def ffn_in_items(gi, hT, hkey, wl_at=(1, 1)):
            f0, n = GROUPS[gi]
            par = gi % 2
            items = []

            def wl():
                S.op("pool", lambda e: e.dma_start(out=WA[:, 4 * par:4 * par + n, :], in_=w_ffn_out_v[:, f0:f0 + n, :]),
                     writes=[("WA", 4 * par + i) for i in range(4)], dma_sem=("WA", 4 * par))
            slots = {}
            for i in range(n):
                for half in range(2):
                    def f(i=i, half=half):
                        fidx = f0 + i
                        rslot = 12 + 4 * par + i
                        if (i, half) == wl_at:
                            wl()
                        if half == 0:
                            slots[i] = (load_block("fi", 128 * fidx, 8),
                                        load_block("fi", D_FF + 128 * fidx, 8))
                        sG, sU = slots[i]
                        p = nunit()
                        th = scr[:, 14 + p, :]
                        kth = ("scr", 14 + p)
                        bG, bU = nbank(), nbank()
                        proj(bG, sG, hT, hkey, half)
                        proj(bU, sU, hT, hkey, half)
                        if half == 1:
                            close_block(sG)
                            close_block(sU)
                        S.op("act", lambda e: e.activation(out=th[:, 0:512], in_=ps[bG][:, :], func=AF.Tanh, scale=0.5),
                             reads=[("ps", bG)], writes=[kth])
                        S.op("dve", lambda e: e.scalar_tensor_tensor(out=th[:, 0:512], in0=th[:, 0:512], scalar=1.0, in1=ps[bG][:, :],
                                                                     op0=ALU.add, op1=ALU.mult),
                             reads=[kth, ("ps", bG)], writes=[kth])
                        S.op("dve", lambda e: e.tensor_tensor(
                            out=R[:, rslot, half * 512:(half + 1) * 512], in0=th[:, 0:512], in1=ps[bU][:, :], op=ALU.mult),
                            reads=[kth, ("ps", bU)], writes=[("R", rslot, half)])
                    items.append(f)
            return items

        def ffn_out_items(gi, out_rows_fn, last):
            f0, n = GROUPS[gi]
            par = gi % 2
            items = []
            fpend = []
            for t in range(8):
                def f(t=t):
                    half = t // 4
                    for c in range(2):
                        b = nbank()
                        for i in range(n):
                            S.op("pe", lambda e, b=b, i=i, c=c: e.matmul(
                                ps[b][:, :], lhsT=R[:, 12 + 4 * par + i, t * 128:(t + 1) * 128],
                                rhs=WA[:, 4 * par + i, c * 512:(c + 1) * 512], start=(i == 0), stop=(i == n - 1)),
                                reads=[("R", 12 + 4 * par + i, half), ("WA", 4 * par + i)], writes=[("ps", b)])
                        S.op("dve", lambda e, b=b, c=c: e.scalar_tensor_tensor(
                            out=X[:, t, c * 512:(c + 1) * 512], in0=ps[b][:, :], scalar=0.5, in1=X[:, t, c * 512:(c + 1) * 512],
                            op0=ALU.mult, op1=ALU.add),
                            reads=[("ps", b), ("X", t)], writes=[("X", t)])
                    if last:
                        hb = cnt["hbf"] % 2
                        cnt["hbf"] += 1
                        fin = norm_core(X[:, t, :], ("X", t), gFs, "gFs", X[:, t, :], ("X", t), hb)

                        def done(fin=fin, t=t):
                            fin()
                            S.op("sp", lambda e: e.dma_start(out=out_rows_fn(t), in_=X[:, t, :]), reads=[("X", t)], dma_sem=("xo", t))
                        if fpend:
                            fpend.pop(0)()
                        fpend.append(done)
                items.append(f)
            if last:
                def flush():
                    while fpend:
                        fpend.pop(0)()
                items.append(flush)
            return items

        def ffn_items(hT, hkey, out_rows_fn, split_last=False):
            items = list(ffn_in_items(0, hT, hkey))
            for gi in range(1, 6):
                if gi == 5 and split_last:
                    a_ = ffn_in_items(gi, hT, hkey, wl_at=(0, 0))
                    h0 = [a_[2 * i] for i in range(len(a_) // 2)]
                    h1 = [a_[2 * i + 1] for i in range(len(a_) // 2)]
                    b_ = ffn_out_items(gi - 1, None, False)
                    o5 = ffn_out_items(5, out_rows_fn, True)
                    seq = [h0[0], b_[0], b_[1], b_[2], h0[1], b_[3], b_[4], b_[5], h0[2], b_[6], b_[7],
                           h1[0], o5[0], o5[1], h1[1], o5[2], o5[3], h1[2]] + o5[4:]
                    items.extend(seq)
                    return items
                a_ = ffn_in_items(gi, hT, hkey)
                b_ = ffn_out_items(gi - 1, None, False)
                k = max(len(a_), len(b_))
                for q in range(k):
                    if q < len(a_):
                        items.append(a_[q])
                    if q < len(b_):
                        items.append(b_[q])
            items.extend(ffn_out_items(5, out_rows_fn, True))
            return items

        def spread(items, rlist):
            out = {}
            n = len(rlist)
            i = 0
            for k, r in enumerate(rlist):
                t = (len(items) * (k + 1) + n - 1) // n
                out[r] = items[i:t]
                i = t
            return out

        def run_merged(rounds, bg_exp_items, bg_sqrt_items):
            nr = len(rounds)
            ie = iq = 0
            c2_rounds = [r for r in range(nr) if 0 <= r - 2 < 16 and (r - 2) % 2 == 1]
            for r, rd in enumerate(rounds):
                te = (len(bg_exp_items) * (r + 1) + nr - 1) // nr
                if r in c2_rounds:
                    k = c2_rounds.index(r) + 1
                    tq = (len(bg_sqrt_items) * k + len(c2_rounds) - 1) // len(c2_rounds)
                else:
                    tq = iq
                rd(bg_exp_items[ie:te], bg_sqrt_items[iq:tq])
                ie, iq = te, tq
            for it in bg_exp_items[ie:]:
                it()
            for it in bg_sqrt_items[iq:]:
                it()

        HA, HB = (h1T, "h1T"), (h2T, "h2T")
        for it in stage0_items(xp, 0, *HA, use_x=True):
            it()
        late_loads()
        late_consts()
        rounds = rnn_rounds([(HA[0], HA[1], True, False), (HB[0], HB[1], True, False), (HA[0], HA[1], False, True)])
        c2 = [r for r in range(len(rounds)) if 0 <= r - 2 < 48 and (r - 2) % 2 == 1]
        bgq = {}
        bgq.update(spread(stage0_items(xp, NT, *HB, use_x=True), [r for r in c2 if 2 <= r <= 14]))
        bg1 = (stage0_items(xc, 0, *HA, use_x=True) + pool_items(HB[0], HB[1], [1], True, False, (14, 15, 16)))
        bgq.update(spread(bg1, [r for r in c2 if 19 <= r <= 31]))
        gl = gelu_items(*HA)
        pitems = pool_items(HA[0], HA[1], [0, 1], False, True, (14, 15, 16))
        pq = spread(pitems, [r for r in c2 if 35 <= r <= 49 and (r - 33) % 4 != 0])
        for k in range(4):
            bgq[33 + 4 * k] = [gl[k]]
        for r, its in pq.items():
            bgq[r] = bgq.get(r, []) + its
        for r, rd in enumerate(rounds):
            rd([], bgq.get(r, []))

        for sci in (2, 3):
            r0 = (sci % 2) * NT
            for m in range(8):
                load_wo_part(m)
                mix_block(m, *HA)
            prev = None
            tq = []
            for t in range(8):
                def mid():
                    if tq:
                        phb, pt = tq.pop(0)
                        transposes(phb, HB[0], HB[1], pt)
                hb, fin = wo_tile(xc[r0 + t * 128:r0 + (t + 1) * 128, :], t, mid)
                if prev is not None:
                    prev[0]()
                    tq.append((prev[1], prev[2]))
                prev = (fin, hb, t)
            if tq:
                phb, pt = tq.pop(0)
                transposes(phb, HB[0], HB[1], pt)
            prev[0]()
            transposes(prev[1], HB[0], HB[1], prev[2])
            fitems = ffn_items(HB[0], HB[1], lambda t, r0=r0: out_d[r0 + t * 128:r0 + (t + 1) * 128, :],
                               split_last=(sci == 3))
            if sci == 2:
                pre = stage0_items(xc, NT, *HA) + pool_items(HA[0], HA[1], [0, 1], False, False, (0, 1, 2)) + gelu_items(*HA)
                rounds = rnn_rounds([(HA[0], HA[1], False, False)])
                n_pre = 34
                k = 0
                for q, it in enumerate(pre):
                    lo = (n_pre * q) // len(pre)
                    hi = (n_pre * (q + 1)) // len(pre)
                    for z in fitems[lo:hi]:
                        z()
                    it()
                run_merged(rounds, fitems[n_pre:], [])
            else:
                for it in fitems:
                    it()

        S.finalize()
        sems = {e: es.enter_context(nc.semaphore("s_" + e)) for e in ("pe", "act", "dve", "pool", "sp")}
        dsems = {}
        for i, k in enumerate(S.dma_keys):
            dsems[k] = es.enter_context(nc.semaphore("d%d" % i))
        block = es.enter_context(nc.Block())

        @block.sync
        def _(e):
            S.emit_engine("sp", e, sems, dsems)
            for k in S.dma_keys:
                if k[0] == "xo":
                    e.wait_ge(dsems[k], S.final_counters[("D", k)])

        @block.scalar
        def _(e):
            S.emit_engine("act", e, sems, dsems)

        @block.vector
        def _(e):
            S.emit_engine("dve", e, sems, dsems)

        @block.gpsimd
        def _(e):
            S.emit_engine("pool", e, sems, dsems)

        @block.tensor
        def _(e):
            S.emit_engine("pe", e, sems, dsems)
    return nc, load_log


_WINDOWS = (2, 4, 8, 16)


def kernel(x, norm_mix, w_in, w_pool_grp, pool_scale, w_pool_out, conv_w, conv_b,
           w_rg_a, b_rg_a, w_rg_x, b_rg_x, lru_lambda, w_rnn_out, w_o, norm_ffn,
           w_ffn_in, w_ffn_out, norm_final):
    f = lambda a: np.ascontiguousarray(np.asarray(a, dtype=np.float32))
    x = f(x)
    B, SEQ, _ = x.shape
    common = {
        "norm_mix": f(norm_mix)[0], "w_in": f(w_in)[0], "w_pool_grp": f(w_pool_grp)[0], "pool_scale": f(pool_scale)[0],
        "w_pool_out": f(w_pool_out)[0], "conv_w": f(conv_w)[0], "conv_b": f(conv_b)[0], "w_rg_a": f(w_rg_a)[0],
        "b_rg_a": f(b_rg_a)[0], "w_rg_x": f(w_rg_x)[0], "b_rg_x": f(b_rg_x)[0], "lru_lambda": f(lru_lambda)[0],
        "w_rnn_out": f(w_rnn_out)[0], "w_o": f(w_o)[0], "norm_ffn": f(norm_ffn)[0], "w_ffn_in": f(w_ffn_in)[0],
        "w_ffn_out": f(w_ffn_out)[0], "norm_final": f(norm_final),
        "idn": np.eye(128, dtype=np.float32),
    }
    invc = np.zeros((2, 128, 64), np.float32)
    for g, w in enumerate(_WINDOWS):
        for t in range(16):
            invc[0, :, g * 16 + t] = 1.0 / min(t + 1, w)
            invc[1, :, g * 16 + t] = 1.0 / w
    in_maps = []
    for c in range(8):
        b, h = c // 2, c % 2
        m = dict(common)
        m["xc"] = np.ascontiguousarray(x[b, h * TOK:(h + 1) * TOK, :])
        m["xp"] = np.ascontiguousarray(x[b, 0:TOK, :]) if h == 1 else np.zeros((TOK, D), np.float32)
        m["flag"] = np.full((128, 1), float(h), np.float32)
        m["invc"] = invc[h]
        in_maps.append(m)
    _, log = build_program(None)
    nc, _ = build_program(log)
    res = run_bass_kernel_spmd(nc, in_maps, core_ids=list(range(8)))
    out = np.empty((B, SEQ, D), np.float32)
    for c in range(8):
        b, h = c // 2, c % 2
        out[b, h * TOK:(h + 1) * TOK, :] = res.results[c]["out"]
    return out
```

```python
from contextlib import ExitStack

import numpy as np
import concourse.bass as bass
import concourse.mybir as mybir
from concourse.bass_utils import run_bass_kernel_spmd

F32 = mybir.dt.float32
BF16 = mybir.dt.bfloat16
AF = mybir.ActivationFunctionType
ALU = mybir.AluOpType

D = 1024
TOK = 2048
NT = 1024
D_IN = 4608
D_FF = 2816
NF = 22
EPS = 1e-6


class _Op:
    __slots__ = ("eng", "fn", "deps", "sig", "sem_key", "inc", "value")

    def __init__(self, eng, fn):
        self.eng = eng
        self.fn = fn
        self.deps = []
        self.sig = False
        self.sem_key = None
        self.inc = 1
        self.value = None


class Sched:
    def __init__(self, same_engine_sync=True):
        self.ops = []
        self.last_w = {}
        self.readers = {}
        self.same_engine_sync = same_engine_sync
        self.dma_keys = []
        self.final_counters = {}

    def op(self, eng, fn, reads=(), writes=(), dma_sem=None):
        o = _Op(eng, fn)
        if dma_sem is not None:
            o.sem_key = dma_sem
            o.inc = 16
            o.sig = True
            if dma_sem not in self.dma_keys:
                self.dma_keys.append(dma_sem)
        deps = []
        for k in reads:
            w = self.last_w.get(k)
            if w is not None:
                deps.append(w)
        for k in writes:
            w = self.last_w.get(k)
            if w is not None:
                deps.append(w)
            deps.extend(self.readers.get(k, ()))
        seen = set()
        for d in deps:
            if d is o or id(d) in seen:
                continue
            seen.add(id(d))
            if d.sem_key is None and d.eng == eng:
                if eng == "pe" or not self.same_engine_sync:
                    continue
            o.deps.append(d)
            d.sig = True
        for k in reads:
            self.readers.setdefault(k, []).append(o)
        for k in writes:
            self.last_w[k] = o
            self.readers[k] = []
        self.ops.append(o)
        return o

    def finalize(self):
        counters = {}
        for o in self.ops:
            if o.sig:
                key = ("E", o.eng) if o.sem_key is None else ("D", o.sem_key)
                counters[key] = counters.get(key, 0) + o.inc
                o.value = counters[key]
        self.final_counters = counters

    def emit_engine(self, eng_name, eng, sems, dma_sems):
        waited = {}
        for o in self.ops:
            if o.eng != eng_name:
                continue
            for d in o.deps:
                key = ("E", d.eng) if d.sem_key is None else ("D", d.sem_key)
                if waited.get(key, 0) >= d.value:
                    continue
                sem = sems[d.eng] if d.sem_key is None else dma_sems[d.sem_key]
                eng.wait_ge(sem, d.value)
                waited[key] = d.value
            ins = o.fn(eng)
            if o.sig:
                sem = sems[o.eng] if o.sem_key is None else dma_sems[o.sem_key]
                ins.then_inc(sem, o.inc)


PF_D = 3


def build_program(prefetch=None):
    load_log = []
    nc = bass.Bass("TRN2", target_bir_lowering=False)

    def din(name, shape):
        return nc.dram_tensor(name, list(shape), F32, kind="ExternalInput").ap()

    xc = din("xc", [TOK, D])
    xp = din("xp", [TOK, D])
    flag_d = din("flag", [128, 1])
    invc_d = din("invc", [128, 64])
    idn_d = din("idn", [128, 128])
    norm_mix = din("norm_mix", [D])
    w_in = din("w_in", [D, D_IN])
    w_pool_grp = din("w_pool_grp", [4, 128, 128])
    pool_scale = din("pool_scale", [512])
    w_pool_out = din("w_pool_out", [512, D])
    conv_w = din("conv_w", [4, D])
    conv_b = din("conv_b", [D])
    w_rg_a = din("w_rg_a", [8, 128, 128])
    b_rg_a = din("b_rg_a", [8, 128])
    w_rg_x = din("w_rg_x", [8, 128, 128])
    b_rg_x = din("b_rg_x", [8, 128])
    lru_lambda = din("lru_lambda", [D])
    w_rnn_out = din("w_rnn_out", [D, D])
    w_o = din("w_o", [D, D])
    norm_ffn = din("norm_ffn", [D])
    w_ffn_in = din("w_ffn_in", [D, 2 * D_FF])
    w_ffn_out = din("w_ffn_out", [D_FF, D])
    norm_final = din("norm_final", [D])
    out_d = nc.dram_tensor("out", [TOK, D], F32, kind="ExternalOutput").ap()

    w_in_v = w_in.rearrange("(kc p) n -> p kc n", p=128)
    w_ffn_in_v = w_ffn_in.rearrange("(kc p) n -> p kc n", p=128)
    w_ffn_out_v = w_ffn_out.rearrange("(f p) n -> p f n", p=128)
    w_o_v = w_o.rearrange("(kc p) n -> p kc n", p=128)
    w_rnn_out_v = w_rnn_out.rearrange("(kc p) n -> p kc n", p=128)
    w_pool_out_v = w_pool_out.rearrange("(kc p) n -> p kc n", p=128)

    S = Sched(same_engine_sync=True)

    with ExitStack() as es:
        def sb(name, shape, dt=F32):
            return es.enter_context(nc.sbuf_tensor(name, list(shape), dt))

        xs0 = sb("xs0", [128, 2, D])
        hbf = sb("hbf", [128, 2, D], BF16)
        X = sb("X", [128, 8, D])
        h1T = sb("h1T", [128, 8, NT], BF16)
        h2T = sb("h2T", [128, 8, NT], BF16)
        R = sb("R", [128, 20, NT], BF16)
        WA = sb("WA", [128, 8, D], BF16)
        rga = sb("rga", [128, 8, 128], BF16)
        rgx = sb("rgx", [128, 8, 128], BF16)
        grp = sb("grp", [128, 4, 128], BF16)
        ring = sb("ring", [128, 8, 8, 128], BF16)
        NSCR = 17
        scr = sb("scr", [128, NSCR, 528])
        vb = sb("vb", [128, 2, 512], BF16)
        pl = sb("pl", [128, 2, 512], BF16)
        g1s = sb("g1s", [128, D])
        g2s = sb("g2s", [128, D])
        gFs = sb("gFs", [128, D])
        identf = sb("identf", [128, 128])
        ident = sb("ident", [128, 128], BF16)
        cw = sb("cw", [128, 4, 8])
        cbt = sb("cbt", [128, 8])
        hba = sb("hba", [128, 8])
        hbx = sb("hbx", [128, 8])
        lam = sb("lam", [128, 8])
        s_t = sb("s_t", [128, 8])
        hs_t = sb("hs_t", [128, 8])
        psc = sb("psc", [128, 4])
        invct = sb("invct", [128, 4, 16])
        flagt = sb("flagt", [128, 1])
        eps_t = sb("eps_t", [128, 1])
        q25 = sb("q25", [128, 1])
        one_t = sb("one_t", [128, 1])
        uh = sb("uh", [128, 8, 3])
        ph = sb("ph", [128, 4, 15])
        hst = sb("hst", [128, 8])
        NSS = 16
        ssq = sb("ssq", [128, NSS])
        sd = sb("sd", [128, NSS])
        rs = sb("rs", [128, NSS])
        umb = sb("umb", [128, 1])

        ps = [es.enter_context(nc.psum_tensor("ps%d" % i, [128, 512], F32)) for i in range(7)]
        pst = es.enter_context(nc.psum_tensor("pst", [128, 8, 128], BF16))

        cnt = {"ring": 0, "bank": 0, "ss": 0, "xs0": 0, "hbf": 0, "unit": 0, "plu": 0}

        def nbank():
            b = cnt["bank"] % 7
            cnt["bank"] += 1
            return b

        def nss():
            n = cnt["ss"] % NSS
            cnt["ss"] += 1
            return n

        def nunit():
            p = cnt["unit"] % 2
            cnt["unit"] += 1
            return p

        views = {"in": w_in_v, "po": w_pool_out_v, "ro": w_rnn_out_v, "fi": w_ffn_in_v}
        NRING = 8
        lstate = {"emitted": 0, "use": 0}
        open_slots = set()

        def emit_load(desc, slot):
            v, c0, K = desc
            src = views[v][:, :, c0:c0 + 128]
            S.op("pool", lambda e: e.dma_start(out=ring[:, slot, 0:K, :], in_=src),
                 writes=[("ring", slot)], dma_sem=("ring", slot))
            open_slots.add(slot)

        def pump_loads(upto):
            while lstate["emitted"] <= min(upto, len(prefetch) - 1):
                k = lstate["emitted"]
                if (k % NRING) in open_slots:
                    break
                emit_load(prefetch[k], k % NRING)
                lstate["emitted"] += 1

        def load_block(v, c0, K):
            desc = (v, c0, K)
            i = lstate["use"]
            lstate["use"] += 1
            load_log.append(desc)
            if prefetch is None:
                assert (i % NRING) not in open_slots, "ring slot still in use"
                emit_load(desc, i % NRING)
            else:
                assert prefetch[i] == desc
                pump_loads(i + PF_D)
                assert lstate["emitted"] > i, "load %d could not be issued: ring slot busy" % i
            return i % NRING

        def close_block(slot):
            open_slots.discard(slot)
            if prefetch is not None:
                pump_loads(lstate["use"] - 1 + PF_D)

        def proj(bank, slot, hT, hkey, half, K=8):
            rk = [(hkey, 4 * half + i) for i in range(4)]
            for kc in range(K):
                S.op("pe", lambda e, bank=bank, slot=slot, kc=kc, half=half, hT=hT, K=K: e.matmul(
                    ps[bank][:, :], lhsT=ring[:, slot, kc, :], rhs=hT[:, kc, half * 512:(half + 1) * 512],
                    start=(kc == 0), stop=(kc == K - 1)),
                    reads=[("ring", slot)] + rk, writes=[("ps", bank)])

        def norm_core(src_ap, src_key, gtile, gkey, out_ap, out_key, jhb, extra_reads=()):
            n = nss()
            S.op("act", lambda e, n=n: e.activation(out=hbf[:, jhb, :], in_=src_ap, func=AF.Square, accum_out=ssq[:, n:n + 1]),
                 reads=[src_key], writes=[("ssq", n), ("hbf", jhb)])
            S.op("act", lambda e, n=n: e.activation(out=sd[:, n:n + 1], in_=ssq[:, n:n + 1], func=AF.Sqrt,
                                                    scale=1.0 / D, bias=eps_t[:]),
                 reads=[("ssq", n), "m_eps"], writes=[("sd", n)])
            def fin():
                S.op("dve", lambda e, n=n: e.reciprocal(out=rs[:, n:n + 1], in_=sd[:, n:n + 1]),
                     reads=[("sd", n)], writes=[("rs", n)])
                S.op("dve", lambda e, n=n: e.scalar_tensor_tensor(out=out_ap, in0=src_ap, scalar=rs[:, n:n + 1], in1=gtile[:],
                                                                  op0=ALU.mult, op1=ALU.mult),
                     reads=[src_key, ("rs", n), gkey] + list(extra_reads), writes=[out_key])
            return fin

        def transposes(hb, hT, hkey, t):
            for k in range(8):
                S.op("pe", lambda e, k=k, hb=hb: e.transpose(out=pst[:, k, :], in_=hbf[:, hb, k * 128:(k + 1) * 128],
                                                             identity=ident[:]),
                     reads=[("hbf", hb), "ident"], writes=[("pst",)])
            S.op("act", lambda e, t=t, hT=hT: e.activation(out=hT[:, :, t * 128:(t + 1) * 128], in_=pst[:], func=AF.Copy),
                 reads=[("pst",)], writes=[(hkey, t)])

        def small_load(dst, src, key, slow=False):
            kw = {"allow_slow_non_contiguous": True} if slow else {}
            S.op("sp", lambda e: e.dma_start(out=dst, in_=src, **kw), writes=[key], dma_sem=("c", key))

        small_load(identf[:], idn_d, "identf")
        small_load(g1s[:], norm_mix.partition_broadcast(128), "g1s")

        def late_loads():
            small_load(flagt[:], flag_d, "flagt")
            small_load(invct[:], invc_d.rearrange("p (g t) -> p g t", g=4), "invct")
            for k in range(4):
                small_load(cw[:, k, :], conv_w[k, :].rearrange("(j p) -> p j", p=128), "cw%d" % k, slow=True)
            small_load(cbt[:], conv_b.rearrange("(j p) -> p j", p=128), "cbt", slow=True)
            small_load(hba[:], b_rg_a.rearrange("j p -> p j"), "hba", slow=True)
            small_load(hbx[:], b_rg_x.rearrange("j p -> p j"), "hbx", slow=True)
            small_load(lam[:], lru_lambda.rearrange("(j p) -> p j", p=128), "lam", slow=True)
            small_load(psc[:], pool_scale.rearrange("(g p) -> p g", p=128), "psc", slow=True)
            small_load(g2s[:], norm_ffn.partition_broadcast(128), "g2s")
            small_load(gFs[:], norm_final.partition_broadcast(128), "gFs")
        S.op("pool", lambda e: e.dma_start(out=rga[:], in_=w_rg_a.rearrange("j i o -> i j o")), writes=["rga"], dma_sem=("c", "rga"))
        S.op("pool", lambda e: e.dma_start(out=rgx[:], in_=w_rg_x.rearrange("j i o -> i j o")), writes=["rgx"], dma_sem=("c", "rgx"))
        S.op("pool", lambda e: e.dma_start(out=grp[:], in_=w_pool_grp.rearrange("g c d -> c g d")), writes=["grp"], dma_sem=("c", "grp"))

        S.op("dve", lambda e: e.memset(eps_t[:], EPS), writes=["m_eps"])
        S.op("dve", lambda e: e.memset(q25[:], 0.25), writes=["m_q25"])
        S.op("dve", lambda e: e.memset(one_t[:], 1.0), writes=["m_one"])
        S.op("dve", lambda e: e.memset(uh[:], 0.0), writes=[("uh", j) for j in range(8)])
        S.op("dve", lambda e: e.memset(ph[:], 0.0), writes=[("ph", g) for g in range(4)])
        S.op("dve", lambda e: e.memset(hst[:], 0.0), writes=[("hst", j) for j in range(8)])
        S.op("dve", lambda e: e.tensor_copy(out=ident[:], in_=identf[:]), reads=["identf"], writes=["ident"])
        def late_consts():
            S.op("dve", lambda e: e.tensor_scalar(out=hba[:], in0=hba[:], scalar1=0.5, scalar2=None, op0=ALU.mult),
                 reads=["hba"], writes=["hba"])
            S.op("dve", lambda e: e.tensor_scalar(out=hbx[:], in0=hbx[:], scalar1=0.5, scalar2=None, op0=ALU.mult),
                 reads=["hbx"], writes=["hbx"])
            S.op("act", lambda e: e.activation(out=lam[:], in_=lam[:], func=AF.Exp, scale=-1.0), reads=["lam"], writes=["lam"])
            S.op("act", lambda e: e.activation(out=lam[:], in_=lam[:], func=AF.Ln, bias=one_t[:]), reads=["lam", "m_one"], writes=["lam"])
            S.op("dve", lambda e: e.tensor_scalar(out=s_t[:], in0=lam[:], scalar1=-8.0, scalar2=None, op0=ALU.mult),
                 reads=["lam"], writes=["s_t"])
            S.op("dve", lambda e: e.tensor_scalar(out=hs_t[:], in0=lam[:], scalar1=-4.0, scalar2=None, op0=ALU.mult),
                 reads=["lam"], writes=["hs_t"])
            S.op("dve", lambda e: e.memset(umb[:], 0.0),
                 reads=["identf", "flagt", "invct", "cw0", "cw1", "cw2", "cw3", "cbt", "hba", "hbx", "psc", "g1s", "g2s", "gFs", "rga", "rgx", "grp",
                        "m_eps", "m_q25", "m_one", "ident", "s_t", "hs_t"],
                 writes=["consts"])


        def stage0_items(src, r0, hT, hkey, use_x=False):
            items = []
            st = {"prev": None}

            def mk(t):
                def f():
                    hb = cnt["hbf"] % 2
                    cnt["hbf"] += 1
                    rows = src[r0 + t * 128:r0 + (t + 1) * 128, :]
                    if use_x:
                        land, lkey = X[:, t, :], ("X", t)
                    else:
                        sl = cnt["xs0"] % 2
                        cnt["xs0"] += 1
                        land, lkey = xs0[:, sl, :], ("xs0", sl)
                    if st["tq"]:
                        phb, pt = st["tq"].pop(0)
                        transposes(phb, hT, hkey, pt)
                    S.op("sp", lambda e: e.dma_start(out=land, in_=rows), writes=[lkey], dma_sem=lkey)
                    fin = norm_core(land, lkey, g1s, "g1s", hbf[:, hb, :], ("hbf", hb), hb)
                    if st["prev"] is not None:
                        pfin, phb2, pt2 = st["prev"]
                        pfin()
                        st["tq"].append((phb2, pt2))
                    st["prev"] = (fin, hb, t)
                return f

            st["tq"] = []
            for t in range(8):
                items.append(mk(t))

            def last1():
                if st["tq"]:
                    phb, pt = st["tq"].pop(0)
                    transposes(phb, hT, hkey, pt)
                pfin, phb2, pt2 = st["prev"]
                pfin()
                st["tq"].append((phb2, pt2))

            def last2():
                phb, pt = st["tq"].pop(0)
                transposes(phb, hT, hkey, pt)
            items.append(last1)
            items.append(last2)
            return items

        def rnn_rounds(segs):
            slots = {}

            def bufs(u):
                return dict(
                    ub=(scr[:, u % 2, :], ("scr", u % 2)),
                    v=(scr[:, 2 + u % 2, :], ("scr", 2 + u % 2)),
                    thr=(scr[:, 4, :], ("scr", 4)),
                    w=(scr[:, 5 + u % 3, :], ("scr", 5 + u % 3)),
                    a=(scr[:, 8 + u % 3, :], ("scr", 8 + u % 3)),
                    a2=(scr[:, 11 + u % 2, :], ("scr", 11 + u % 2)),
                    hs=(scr[:, 13, :], ("scr", 13)),
                )

            def phA(u):
                hT, hkey = segs[u // 16][0], segs[u // 16][1]
                j, half = (u % 16) // 2, u % 2
                if half == 0:
                    slots[j] = load_block("in", 512 + 128 * j, 8)
                ub, kub = bufs(u)["ub"]
                b0 = nbank()
                proj(b0, slots[j], hT, hkey, half)
                if half == 1:
                    close_block(slots[j])
                S.op("act", lambda e: e.activation(out=ub[:, 3:515], in_=ps[b0][:, :], func=AF.Copy),
                     reads=[("ps", b0)], writes=[kub])

            def phB(u):
                j, half = (u % 16) // 2, u % 2
                ub, kub = bufs(u)["ub"]
                v, kv = bufs(u)["v"]
                S.op("dve", lambda e: e.tensor_copy(out=ub[:, 0:3], in_=uh[:, j, :]), reads=[("uh", j)], writes=[kub])
                S.op("dve", lambda e: e.tensor_copy(out=uh[:, j, :], in_=ub[:, 512:515]), reads=[kub], writes=[("uh", j)])
                S.op("dve", lambda e: e.tensor_scalar(out=v[:, 0:512], in0=ub[:, 0:512], scalar1=cw[:, 0, j:j + 1],
                                                      scalar2=cbt[:, j:j + 1], op0=ALU.mult, op1=ALU.add),
                     reads=[kub, "consts"], writes=[kv])
                for k in range(1, 4):
                    S.op("dve", lambda e, k=k: e.scalar_tensor_tensor(
                        out=v[:, 0:512], in0=ub[:, k:k + 512], scalar=cw[:, k, j:j + 1], in1=v[:, 0:512],
                        op0=ALU.mult, op1=ALU.add), reads=[kub, kv], writes=[kv])

            def phVB(u):
                p = u % 2
                v, kv = bufs(u)["v"]
                S.op("act", lambda e: e.activation(out=vb[:, p, :], in_=v[:, 0:512], func=AF.Copy), reads=[kv], writes=[("vb", p)])

            def phC(u):
                j, half = (u % 16) // 2, u % 2
                p = u % 2
                B = bufs(u)
                v, kv = B["v"]
                thr, kthr = B["thr"]
                w, kw = B["w"]
                a, ka = B["a"]
                b1 = nbank()
                b2 = nbank()
                S.op("pe", lambda e: e.matmul(ps[b1][:, :], lhsT=rga[:, j, :], rhs=vb[:, p, :], start=True, stop=True),
                     reads=[("vb", p), "consts"], writes=[("ps", b1)])
                S.op("pe", lambda e: e.matmul(ps[b2][:, :], lhsT=rgx[:, j, :], rhs=vb[:, p, :], start=True, stop=True),
                     reads=[("vb", p), "consts"], writes=[("ps", b2)])
                S.op("act", lambda e: e.activation(out=thr[:, 0:512], in_=ps[b1][:, :], func=AF.Tanh,
                                                   scale=0.5, bias=hba[:, j:j + 1]),
                     reads=[("ps", b1)], writes=[kthr])
                S.op("act", lambda e: e.activation(out=w[:, 0:512], in_=ps[b2][:, :], func=AF.Tanh,
                                                   scale=0.5, bias=hbx[:, j:j + 1]),
                     reads=[("ps", b2)], writes=[kw])
                S.op("act", lambda e: e.activation(out=a[:, 0:512], in_=thr[:, 0:512], func=AF.Exp,
                                                   scale=hs_t[:, j:j + 1], bias=hs_t[:, j:j + 1]),
                     reads=[kthr], writes=[ka])
                S.op("dve", lambda e: e.scalar_tensor_tensor(out=w[:, 0:512], in0=w[:, 0:512], scalar=1.0,
                                                             in1=v[:, 0:512], op0=ALU.add, op1=ALU.mult),
                     reads=[kw, kv], writes=[kw])

            def phC2(u, part):
                B = bufs(u)
                a, ka = B["a"]
                a2, ka2 = B["a2"]
                if part == 0:
                    S.op("act", lambda e: e.activation(out=a2[:, 0:512], in_=a[:, 0:512], func=AF.Square),
                         reads=[ka], writes=[ka2])
                else:
                    S.op("act", lambda e: e.activation(out=a2[:, 0:512], in_=a2[:, 0:512], func=AF.Sqrt,
                                                       scale=-0.25, bias=q25[:]),
                         reads=[ka2], writes=[ka2])

            def phD(u):
                state_only, apply_flag = segs[u // 16][2], segs[u // 16][3]
                j, half = (u % 16) // 2, u % 2
                B = bufs(u)
                if apply_flag and half == 0:
                    S.op("dve", lambda e: e.tensor_scalar(out=hst[:, j:j + 1], in0=hst[:, j:j + 1], scalar1=flagt[:, 0:1],
                                                          scalar2=None, op0=ALU.mult),
                         reads=[("hst", j), "consts"], writes=[("hst", j)])
                w, kw = B["w"]
                a, ka = B["a"]
                a2, ka2 = B["a2"]
                hs, khs = B["hs"]
                S.op("dve", lambda e: e.tensor_tensor(out=a2[:, 0:512], in0=a2[:, 0:512], in1=w[:, 0:512], op=ALU.mult),
                     reads=[kw, ka2], writes=[ka2])
                S.op("dve", lambda e: e.tensor_tensor_scan(out=hs[:, 0:512], data0=a[:, 0:512], data1=a2[:, 0:512],
                                                           initial=hst[:, j:j + 1], op0=ALU.mult, op1=ALU.add),
                     reads=[ka, ka2, ("hst", j)], writes=[khs])
                S.op("dve", lambda e: e.tensor_copy(out=hst[:, j:j + 1], in_=hs[:, 511:512]), reads=[khs], writes=[("hst", j)])
                if not state_only:
                    S.op("dve", lambda e: e.tensor_tensor(
                        out=R[:, j, half * 512:(half + 1) * 512], in0=hs[:, 0:512], in1=R[:, j, half * 512:(half + 1) * 512],
                        op=ALU.mult), reads=[khs, ("R", j, half)], writes=[("R", j, half)])

            N = 16 * len(segs)

            def mk_round(r):
                def f(bg_exp=(), bg_sqrt=()):
                    if r < N:
                        phA(r)
                    if 0 <= r - 1 < N:
                        phB(r - 1)
                    if 0 <= r - 2 < N:
                        phC(r - 2)
                    for it in bg_exp:
                        it()
                    if 0 <= r - 2 < N and (r - 2) % 2 == 1:
                        phC2(r - 3, 0)
                        phC2(r - 3, 1)
                        phC2(r - 2, 0)
                        phC2(r - 2, 1)
                    if 0 <= r - 3 < N and (r - 3) % 2 == 1:
                        phD(r - 4)
                        phD(r - 3)
                    for it in bg_sqrt:
                        it()
                    if 0 <= r - 1 < N:
                        phVB(r - 1)
                return f

            return [mk_round(r) for r in range(N + 5)]

        def gelu_items(hT, hkey):
            items = []
            for j0 in range(0, 8, 2):
                def f(j0=j0):
                    for j in (j0, j0 + 1):
                        slot = load_block("in", 1536 + 128 * j, 8)
                        for half in range(2):
                            b0 = nbank()
                            proj(b0, slot, hT, hkey, half)
                            if half == 1:
                                close_block(slot)
                            S.op("act", lambda e, b0=b0, j=j, half=half: e.activation(
                                out=R[:, j, half * 512:(half + 1) * 512], in_=ps[b0][:, :], func=AF.Gelu_apprx_tanh),
                                reads=[("ps", b0)], writes=[("R", j, half)])
                items.append(f)
            return items

        def pool_items(hT, hkey, halves, halo_only, fixup, sb3):
            items = []
            slots = {}
            pend = []
            for g in range(4):
                for half in halves:
                    def f(g=g, half=half):
                        if g not in slots:
                            slots[g] = load_block("in", 128 * g, 8)
                        slot = slots[g]
                        w = 2 ** (g + 1)
                        p = cnt["plu"] % 2
                        cnt["plu"] += 1
                        pb, sA, sB = [scr[:, i, :] for i in sb3]
                        kpb, ksA, ksB = [("scr", i) for i in sb3]
                        b0 = nbank()
                        proj(b0, slot, hT, hkey, half)
                        if half == halves[-1]:
                            close_block(slot)
                        S.op("act", lambda e: e.activation(out=pb[:, 0:15], in_=ph[:, g, :], func=AF.Copy),
                             reads=[("ph", g)], writes=[kpb])
                        S.op("act", lambda e: e.activation(out=pb[:, 15:527], in_=ps[b0][:, :], func=AF.Copy),
                             reads=[("ps", b0)], writes=[kpb])
                        S.op("act", lambda e: e.activation(out=ph[:, g, :], in_=pb[:, 512:527], func=AF.Copy),
                             reads=[kpb], writes=[("ph", g)])
                        if halo_only:
                            return
                        S.op("dve", lambda e: e.tensor_tensor(out=sA[:, 1:527], in0=pb[:, 0:526], in1=pb[:, 1:527], op=ALU.add),
                             reads=[kpb], writes=[ksA])
                        sW, ksW = sA, ksA
                        if g >= 1:
                            S.op("dve", lambda e: e.tensor_tensor(out=sB[:, 3:527], in0=sA[:, 3:527], in1=sA[:, 1:525], op=ALU.add),
                                 reads=[ksA], writes=[ksB])
                            sW, ksW = sB, ksB
                        if g >= 2:
                            S.op("dve", lambda e: e.tensor_tensor(out=sA[:, 7:527], in0=sB[:, 7:527], in1=sB[:, 3:523], op=ALU.add),
                                 reads=[ksB], writes=[ksA])
                            sW, ksW = sA, ksA
                        if g >= 3:
                            S.op("dve", lambda e: e.tensor_tensor(out=sB[:, 15:527], in0=sA[:, 15:527], in1=sA[:, 7:519], op=ALU.add),
                                 reads=[ksA], writes=[ksB])
                            sW, ksW = sB, ksB
                        S.op("dve", lambda e: e.scalar_tensor_tensor(
                            out=pl[:, p, :], in0=sW[:, 15:527], scalar=1.0 / w, in1=pb[:, 15:527], op0=ALU.mult, op1=ALU.subtract),
                            reads=[ksW, kpb], writes=[("pl", p)])
                        if fixup and half == 0:
                            S.op("dve", lambda e: e.tensor_tensor(out=sW[:, 15:31], in0=sW[:, 15:31], in1=invct[:, g, :], op=ALU.mult),
                                 reads=[ksW, "consts"], writes=[ksW])
                            S.op("dve", lambda e: e.tensor_tensor(out=pl[:, p, 0:16], in0=sW[:, 15:31], in1=pb[:, 15:31],
                                                                  op=ALU.subtract),
                                 reads=[ksW, kpb], writes=[("pl", p)])
                        def part2():
                            b1 = nbank()
                            S.op("pe", lambda e: e.matmul(ps[b1][:, :], lhsT=grp[:, g, :], rhs=pl[:, p, :], start=True, stop=True),
                                 reads=[("pl", p), "consts"], writes=[("ps", b1)])
                            S.op("act", lambda e: e.activation(out=R[:, 8 + g, half * 512:(half + 1) * 512], in_=ps[b1][:, :],
                                                               func=AF.Identity, scale=psc[:, g:g + 1]),
                                 reads=[("ps", b1)], writes=[("R", 8 + g, half)])
                        pend.append(part2)

                    def wrapped(f=f):
                        prev = pend.pop(0) if pend else None
                        f()
                        if prev is not None:
                            prev()
                    items.append(wrapped)
            if not halo_only:
                def flush():
                    while pend:
                        pend.pop(0)()
                items.append(flush)
            return items

        def mix_block(m, hT, hkey):
            sGP = load_block("in", 2560 + 128 * m, 8)
            sGR = load_block("in", 3584 + 128 * m, 8)
            sPO = load_block("po", 128 * m, 4)
            sRO = load_block("ro", 128 * m, 8)
            for half in range(2):
                p = nunit()
                base = 7 * p
                thp, thr, t1, t2 = [scr[:, base + i, :] for i in range(4)]
                kthp, kthr, kt1, kt2 = [("scr", base + i) for i in range(4)]
                bGP, bGR, bYP, bYR = nbank(), nbank(), nbank(), nbank()
                proj(bGP, sGP, hT, hkey, half)
                proj(bGR, sGR, hT, hkey, half)
                for g in range(4):
                    S.op("pe", lambda e, g=g, half=half, bYP=bYP, sPO=sPO: e.matmul(
                        ps[bYP][:, :], lhsT=ring[:, sPO, g, :], rhs=R[:, 8 + g, half * 512:(half + 1) * 512],
                        start=(g == 0), stop=(g == 3)), reads=[("ring", sPO), ("R", 8 + g, half)], writes=[("ps", bYP)])
                for j in range(8):
                    S.op("pe", lambda e, j=j, half=half, bYR=bYR, sRO=sRO: e.matmul(
                        ps[bYR][:, :], lhsT=ring[:, sRO, j, :], rhs=R[:, j, half * 512:(half + 1) * 512],
                        start=(j == 0), stop=(j == 7)), reads=[("ring", sRO), ("R", j, half)], writes=[("ps", bYR)])
                if half == 1:
                    for sl_ in (sGP, sGR, sPO, sRO):
                        close_block(sl_)
                S.op("act", lambda e, thp=thp, bGP=bGP: e.activation(out=thp[:, 0:512], in_=ps[bGP][:, :], func=AF.Tanh, scale=0.5),
                     reads=[("ps", bGP)], writes=[kthp])
                S.op("act", lambda e, thr=thr, bGR=bGR: e.activation(out=thr[:, 0:512], in_=ps[bGR][:, :], func=AF.Tanh, scale=0.5),
                     reads=[("ps", bGR)], writes=[kthr])
                S.op("dve", lambda e, thp=thp, t1=t1, bYP=bYP: e.scalar_tensor_tensor(
                    out=t1[:, 0:512], in0=thp[:, 0:512], scalar=1.0, in1=ps[bYP][:, :], op0=ALU.add, op1=ALU.mult),
                    reads=[kthp, ("ps", bYP)], writes=[kt1])
                S.op("dve", lambda e, thr=thr, t2=t2, bYR=bYR: e.scalar_tensor_tensor(
                    out=t2[:, 0:512], in0=thr[:, 0:512], scalar=1.0, in1=ps[bYR][:, :], op0=ALU.add, op1=ALU.mult),
                    reads=[kthr, ("ps", bYR)], writes=[kt2])
                S.op("dve", lambda e, t1=t1, t2=t2, m=m, half=half: e.tensor_tensor(
                    out=R[:, 12 + m, half * 512:(half + 1) * 512], in0=t1[:, 0:512], in1=t2[:, 0:512], op=ALU.add),
                    reads=[kt1, kt2], writes=[("R", 12 + m, half)])

        WO_ORDER = [0, 1, 2, 3, 7, 4, 5, 6]

        def load_wo_part(m):
            kc = WO_ORDER[m]
            S.op("pool", lambda e: e.dma_start(out=WA[:, kc:kc + 1, :], in_=w_o_v[:, kc:kc + 1, :]),
                 writes=[("WA", kc)], dma_sem=("WA", kc))

        def wo_tile(src_rows, t, mid=None, reload=True):
            if reload:
                S.op("sp", lambda e, t=t: e.dma_start(out=X[:, t, :], in_=src_rows), writes=[("X", t)], dma_sem=("X", t))
            half = t // 4
            for c in range(2):
                b = nbank()
                for m in range(8):
                    S.op("pe", lambda e, b=b, m=m, t=t, c=c: e.matmul(
                        ps[b][:, :], lhsT=R[:, 12 + m, t * 128:(t + 1) * 128], rhs=WA[:, m, c * 512:(c + 1) * 512],
                        start=(m == 0), stop=(m == 7)), reads=[("R", 12 + m, half), ("WA", m)], writes=[("ps", b)])
                S.op("dve", lambda e, b=b, t=t, c=c: e.scalar_tensor_tensor(
                    out=X[:, t, c * 512:(c + 1) * 512], in0=ps[b][:, :], scalar=0.5, in1=X[:, t, c * 512:(c + 1) * 512],
                    op0=ALU.mult, op1=ALU.add), reads=[("ps", b), ("X", t)], writes=[("X", t)])
            if mid is not None:
                mid()
            hb = cnt["hbf"] % 2
            cnt["hbf"] += 1
            fin = norm_core(X[:, t, :], ("X", t), g2s, "g2s", hbf[:, hb, :], ("hbf", hb), hb)
            return hb, fin

        GROUPS = [(0, 4), (4, 4), (8, 4), (12, 4), (16, 3), (19, 3)]

        def ffn_in_items(gi, hT, hkey, wl_at=(1, 1)):
            f0, n = GROUPS[gi]
            par = gi % 2
            items = []

            def wl():
                S.op("pool", lambda e: e.dma_start(out=WA[:, 4 * par:4 * par + n, :], in_=w_ffn_out_v[:, f0:f0 + n, :]),
                     writes=[("WA", 4 * par + i) for i in range(4)], dma_sem=("WA", 4 * par))
            slots = {}
            for i in range(n):
                for half in range(2):
                    def f(i=i, half=half):
                        fidx = f0 + i
                        rslot = 12 + 4 * par + i
                        if (i, half) == wl_at:
                            wl()
                        if half == 0:
                            slots[i] = (load_block("fi", 128 * fidx, 8),
                                        load_block("fi", D_FF + 128 * fidx, 8))
                        sG, sU = slots[i]
                        p = nunit()
                        th = scr[:, 14 + p, :]
                        kth = ("scr", 14 + p)
                        bG, bU = nbank(), nbank()
                        proj(bG, sG, hT, hkey, half)
                        proj(bU, sU, hT, hkey, half)
                        if half == 1:
                            close_block(sG)
                            close_block(sU)
                        S.op("act", lambda e: e.activation(out=th[:, 0:512], in_=ps[bG][:, :], func=AF.Tanh, scale=0.5),
                             reads=[("ps", bG)], writes=[kth])
                        S.op("dve", lambda e: e.scalar_tensor_tensor(out=th[:, 0:512], in0=th[:, 0:512], scalar=1.0, in1=ps[bG][:, :],
                                                                     op0=ALU.add, op1=ALU.mult),
                             reads=[kth, ("ps", bG)], writes=[kth])
                        S.op("dve", lambda e: e.tensor_tensor(
                            out=R[:, rslot, half * 512:(half + 1) * 512], in0=th[:, 0:512], in1=ps[bU][:, :], op=ALU.mult),
                            reads=[kth, ("ps", bU)], writes=[("R", rslot, half)])
                    items.append(f)
            return items

        def ffn_out_items(gi, out_rows_fn, last):
            f0, n = GROUPS[gi]
            par = gi % 2
            items = []
            fpend = []
            for t in range(8):
                def f(t=t):
                    half = t // 4
                    for c in range(2):
                        b = nbank()
                        for i in range(n):
                            S.op("pe", lambda e, b=b, i=i, c=c: e.matmul(
                                ps[b][:, :], lhsT=R[:, 12 + 4 * par + i, t * 128:(t + 1) * 128],
                                rhs=WA[:, 4 * par + i, c * 512:(c + 1) * 512], start=(i == 0), stop=(i == n - 1)),
                                reads=[("R", 12 + 4 * par + i, half), ("WA", 4 * par + i)], writes=[("ps", b)])
                        S.op("dve", lambda e, b=b, c=c: e.scalar_tensor_tensor(
                            out=X[:, t, c * 512:(c + 1) * 512], in0=ps[b][:, :], scalar=0.5, in1=X[:, t, c * 512:(c + 1) * 512],
                            op0=ALU.mult, op1=ALU.add),
                            reads=[("ps", b), ("X", t)], writes=[("X", t)])
                    if last:
                        hb = cnt["hbf"] % 2
                        cnt["hbf"] += 1
                        fin = norm_core(X[:, t, :], ("X", t), gFs, "gFs", X[:, t, :], ("X", t), hb)

                        def done(fin=fin, t=t):
                            fin()
                            S.op("sp", lambda e: e.dma_start(out=out_rows_fn(t), in_=X[:, t, :]), reads=[("X", t)], dma_sem=("xo", t))
                        if fpend:
                            fpend.pop(0)()
                        fpend.append(done)
                items.append(f)
            if last:
                def flush():
                    while fpend:
                        fpend.pop(0)()
                items.append(flush)
            return items

        def ffn_items(hT, hkey, out_rows_fn, split_last=False):
            items = list(ffn_in_items(0, hT, hkey))
            for gi in range(1, 6):
                if gi == 5 and split_last:
                    a_ = ffn_in_items(gi, hT, hkey, wl_at=(0, 0))
                    h0 = [a_[2 * i] for i in range(len(a_) // 2)]
                    h1 = [a_[2 * i + 1] for i in range(len(a_) // 2)]
                    b_ = ffn_out_items(gi - 1, None, False)
                    o5 = ffn_out_items(5, out_rows_fn, True)
                    seq = [h0[0], b_[0], b_[1], b_[2], h0[1], b_[3], b_[4], b_[5], h0[2], b_[6], b_[7],
                           h1[0], o5[0], o5[1], h1[1], o5[2], o5[3], h1[2]] + o5[4:]
                    items.extend(seq)
                    return items
                a_ = ffn_in_items(gi, hT, hkey)
                b_ = ffn_out_items(gi - 1, None, False)
                k = max(len(a_), len(b_))
                for q in range(k):
                    if q < len(a_):
                        items.append(a_[q])
                    if q < len(b_):
                        items.append(b_[q])
            items.extend(ffn_out_items(5, out_rows_fn, True))
            return items

        def spread(items, rlist):
            out = {}
            n = len(rlist)
            i = 0
            for k, r in enumerate(rlist):
                t = (len(items) * (k + 1) + n - 1) // n
                out[r] = items[i:t]
                i = t
            return out

        def run_merged(rounds, bg_exp_items, bg_sqrt_items):
            nr = len(rounds)
            ie = iq = 0
            c2_rounds = [r for r in range(nr) if 0 <= r - 2 < 16 and (r - 2) % 2 == 1]
            for r, rd in enumerate(rounds):
                te = (len(bg_exp_items) * (r + 1) + nr - 1) // nr
                if r in c2_rounds:
                    k = c2_rounds.index(r) + 1
                    tq = (len(bg_sqrt_items) * k + len(c2_rounds) - 1) // len(c2_rounds)
                else:
                    tq = iq
                rd(bg_exp_items[ie:te], bg_sqrt_items[iq:tq])
                ie, iq = te, tq
            for it in bg_exp_items[ie:]:
                it()
            for it in bg_sqrt_items[iq:]:
                it()

        HA, HB = (h1T, "h1T"), (h2T, "h2T")
        for it in stage0_items(xp, 0, *HA, use_x=True):
            it()
        late_loads()
        late_consts()
        rounds = rnn_rounds([(HA[0], HA[1], True, False), (HB[0], HB[1], True, False), (HA[0], HA[1], False, True)])
        c2 = [r for r in range(len(rounds)) if 0 <= r - 2 < 48 and (r - 2) % 2 == 1]
        bgq = {}
        bgq.update(spread(stage0_items(xp, NT, *HB, use_x=True), [r for r in c2 if 2 <= r <= 14]))
        bg1 = (stage0_items(xc, 0, *HA, use_x=True) + pool_items(HB[0], HB[1], [1], True, False, (14, 15, 16)))
        bgq.update(spread(bg1, [r for r in c2 if 19 <= r <= 31]))
        gl = gelu_items(*HA)
        pitems = pool_items(HA[0], HA[1], [0, 1], False, True, (14, 15, 16))
        pq = spread(pitems, [r for r in c2 if 35 <= r <= 49 and (r - 33) % 4 != 0])
        for k in range(4):
            bgq[33 + 4 * k] = [gl[k]]
        for r, its in pq.items():
            bgq[r] = bgq.get(r, []) + its
        for r, rd in enumerate(rounds):
            rd([], bgq.get(r, []))

        for sci in (2, 3):
            r0 = (sci % 2) * NT
            for m in range(8):
                load_wo_part(m)
                mix_block(m, *HA)
            prev = None
            tq = []
            for t in range(8):
                def mid():
                    if tq:
                        phb, pt = tq.pop(0)
                        transposes(phb, HB[0], HB[1], pt)
                hb, fin = wo_tile(xc[r0 + t * 128:r0 + (t + 1) * 128, :], t, mid, reload=(sci == 3))
                if prev is not None:
                    prev[0]()
                    tq.append((prev[1], prev[2]))
                prev = (fin, hb, t)
            if tq:
                phb, pt = tq.pop(0)
                transposes(phb, HB[0], HB[1], pt)
            prev[0]()
            transposes(prev[1], HB[0], HB[1], prev[2])
            fitems = ffn_items(HB[0], HB[1], lambda t, r0=r0: out_d[r0 + t * 128:r0 + (t + 1) * 128, :],
                               split_last=(sci == 3))
            if sci == 2:
                pre = stage0_items(xc, NT, *HA) + pool_items(HA[0], HA[1], [0, 1], False, False, (0, 1, 2)) + gelu_items(*HA)
                rounds = rnn_rounds([(HA[0], HA[1], False, False)])
                n_pre = 30
                k = 0
                for q, it in enumerate(pre):
                    lo = (n_pre * q) // len(pre)
                    hi = (n_pre * (q + 1)) // len(pre)
                    for z in fitems[lo:hi]:
                        z()
                    it()
                run_merged(rounds, fitems[n_pre:], [])
            else:
                for it in fitems:
                    it()

        S.finalize()
        sems = {e: es.enter_context(nc.semaphore("s_" + e)) for e in ("pe", "act", "dve", "pool", "sp")}
        dsems = {}
        for i, k in enumerate(S.dma_keys):
            dsems[k] = es.enter_context(nc.semaphore("d%d" % i))
        block = es.enter_context(nc.Block())

        @block.sync
        def _(e):
            S.emit_engine("sp", e, sems, dsems)
            for k in S.dma_keys:
                if k[0] == "xo":
                    e.wait_ge(dsems[k], S.final_counters[("D", k)])

        @block.scalar
        def _(e):
            S.emit_engine("act", e, sems, dsems)

        @block.vector
        def _(e):
            S.emit_engine("dve", e, sems, dsems)

        @block.gpsimd
        def _(e):
            S.emit_engine("pool", e, sems, dsems)

        @block.tensor
        def _(e):
            S.emit_engine("pe", e, sems, dsems)
    return nc, load_log


_WINDOWS = (2, 4, 8, 16)


def kernel(x, norm_mix, w_in, w_pool_grp, pool_scale, w_pool_out, conv_w, conv_b,
           w_rg_a, b_rg_a, w_rg_x, b_rg_x, lru_lambda, w_rnn_out, w_o, norm_ffn,
           w_ffn_in, w_ffn_out, norm_final):
    f = lambda a: np.ascontiguousarray(np.asarray(a, dtype=np.float32))
    x = f(x)
    B, SEQ, _ = x.shape
    common = {
        "norm_mix": f(norm_mix)[0], "w_in": f(w_in)[0], "w_pool_grp": f(w_pool_grp)[0], "pool_scale": f(pool_scale)[0],
        "w_pool_out": f(w_pool_out)[0], "conv_w": f(conv_w)[0], "conv_b": f(conv_b)[0], "w_rg_a": f(w_rg_a)[0],
        "b_rg_a": f(b_rg_a)[0], "w_rg_x": f(w_rg_x)[0], "b_rg_x": f(b_rg_x)[0], "lru_lambda": f(lru_lambda)[0],
        "w_rnn_out": f(w_rnn_out)[0], "w_o": f(w_o)[0], "norm_ffn": f(norm_ffn)[0], "w_ffn_in": f(w_ffn_in)[0],
        "w_ffn_out": f(w_ffn_out)[0], "norm_final": f(norm_final),
        "idn": np.eye(128, dtype=np.float32),
    }
    invc = np.zeros((2, 128, 64), np.float32)
    for g, w in enumerate(_WINDOWS):
        for t in range(16):
            invc[0, :, g * 16 + t] = 1.0 / min(t + 1, w)
            invc[1, :, g * 16 + t] = 1.0 / w
    in_maps = []
    for c in range(8):
        b, h = c // 2, c % 2
        m = dict(common)
        m["xc"] = np.ascontiguousarray(x[b, h * TOK:(h + 1) * TOK, :])
        m["xp"] = np.ascontiguousarray(x[b, 0:TOK, :]) if h == 1 else np.zeros((TOK, D), np.float32)
        m["flag"] = np.full((128, 1), float(h), np.float32)
        m["invc"] = invc[h]
        in_maps.append(m)
    _, log = build_program(None)
    nc, _ = build_program(log)
    res = run_bass_kernel_spmd(nc, in_maps, core_ids=list(range(8)))
    out = np.empty((B, SEQ, D), np.float32)
    for c in range(8):
        b, h = c // 2, c % 2
        out[b, h * TOK:(h + 1) * TOK, :] = res.results[c]["out"]
    return out
```
